# Optimizing a Trainium2 kernel written in Bass

```python
import math
import jax, jax.numpy as jnp
from jax import lax
import numpy as np

D_MODEL = 1024
BATCH = 2
SEQ = 16384
DEPTH = 4

N_MIXERS = 3
EPS = 1e-6
ATTN_GROUPS = ((128, 1), (512, 4), (2048, 16))
ATTN_N_GROUPS = len(ATTN_GROUPS)
ATTN_HEADS_PER_GROUP = 8
ATTN_HEAD_DIM = 64
ATTN_BLOCK = 128
ATTN_GROUP_WIDTH = ATTN_HEADS_PER_GROUP * ATTN_HEAD_DIM
ATTN_IN_WIDTH = ATTN_N_GROUPS * 3 * ATTN_GROUP_WIDTH
CONV_CHANNELS = D_MODEL
CONV_WIDTH = 31
HGRN_EXPAND = 128
HGRN_HEADS = D_MODEL // HGRN_EXPAND
HGRN_KEY_DIM = HGRN_HEADS * HGRN_EXPAND
HGRN_HEAD_V = D_MODEL // HGRN_HEADS
HGRN_VALUE_DIM = HGRN_HEADS * HGRN_HEAD_V
HGRN_CHUNK = 64
FFN_HIDDEN = 2816
FFN_CONV_WIDTH = 3
N_ATTN_LAYERS = len(range(0, DEPTH, N_MIXERS))
N_CONV_LAYERS = len(range(1, DEPTH, N_MIXERS))
N_HGRN_LAYERS = len(range(2, DEPTH, N_MIXERS))

kernel_name = "hybrid_dilated_attn_conformer_hgrn2_trunk"


def rms_norm(x, gain):
    xf = x.astype(jnp.float32)
    y = xf * lax.rsqrt(jnp.mean(xf * xf, axis=-1, keepdims=True) + EPS)
    return (y * gain.astype(jnp.float32)).astype(x.dtype)


def layer_norm(x, gain, bias):
    xf = x.astype(jnp.float32)
    mu = jnp.mean(xf, axis=-1, keepdims=True)
    var = jnp.mean(jnp.square(xf - mu), axis=-1, keepdims=True)
    y = (xf - mu) * lax.rsqrt(var + EPS)
    return (y * gain.astype(jnp.float32) + bias.astype(jnp.float32)).astype(x.dtype)


def causal_depthwise_conv(x, w, b):
    K, C = w.shape
    y = lax.conv_general_dilated(
        x, w[:, None, :].astype(x.dtype), window_strides=(1,), padding=((K - 1, 0),),
        dimension_numbers=('NWC', 'WIO', 'NWC'), feature_group_count=C)
    return y + b.astype(x.dtype)


def dilated_window_attention(q, k, v, window, dilation):
    B, S, H, Dh = q.shape
    span = window // dilation
    unit = dilation * ATTN_BLOCK
    L = -(-S // unit) * unit
    n = L // dilation
    nb = n // ATTN_BLOCK

    def to_blocks(t):
        t = jnp.pad(t, ((0, 0), (0, L - S), (0, 0), (0, 0)))
        t = t.reshape(B, n, dilation, H, Dh).transpose(0, 2, 3, 1, 4)
        return t.reshape(B, dilation, H, nb, ATTN_BLOCK, Dh)

    def with_prev(t):
        prev = jnp.pad(t, ((0, 0), (0, 0), (0, 0), (1, 0), (0, 0), (0, 0)))[:, :, :, :-1]
        return jnp.concatenate([prev, t], axis=-2)

    qb = to_blocks(q)
    kw = with_prev(to_blocks(k))
    vw = with_prev(to_blocks(v))
    s = jnp.einsum('brhnqd,brhnkd->brhnqk', qb, kw).astype(jnp.float32)
    qi = jnp.arange(ATTN_BLOCK)[:, None]
    kj = jnp.arange(2 * ATTN_BLOCK)[None, :]
    dist = qi + ATTN_BLOCK - kj
    band = (dist >= 0) & (dist <= span)
    kpos = jnp.arange(nb)[:, None, None] * ATTN_BLOCK + kj[None] - ATTN_BLOCK
    mask = band[None] & (kpos >= 0)
    s = jnp.where(mask, s, -jnp.inf)
    lse = jax.nn.logsumexp(s, axis=-1)
    p = jnp.exp(s - lse[..., None])
    o = jnp.einsum('brhnqk,brhnkd->brhnqd', p.astype(v.dtype), vw)
    o = o.reshape(B, dilation, H, n, Dh).transpose(0, 3, 1, 2, 4).reshape(B, L, H, Dh)[:, :S]
    lse = lse.reshape(B, dilation, H, n).transpose(0, 3, 1, 2).reshape(B, L, H)[:, :S]
    return o, lse


def dilated_attention_mixer(h, w_in, q_gain, k_gain, w_out):
    B, S, _ = h.shape
    proj = (h @ w_in).reshape(B, S, ATTN_N_GROUPS, 3, ATTN_HEADS_PER_GROUP, ATTN_HEAD_DIM)
    scale = ATTN_HEAD_DIM ** -0.5
    outs, lses = [], []
    for g, (window, dilation) in enumerate(ATTN_GROUPS):
        q = rms_norm(proj[:, :, g, 0], q_gain[g]) * scale
        k = rms_norm(proj[:, :, g, 1], k_gain[g])
        o, lse = dilated_window_attention(q, k, proj[:, :, g, 2], window, dilation)
        outs.append(o)
        lses.append(lse)
    weights = jax.nn.softmax(jnp.stack(lses), axis=0)
    o = jnp.sum(weights[..., None] * jnp.stack(outs).astype(jnp.float32), axis=0)
    return o.astype(h.dtype).reshape(B, S, ATTN_GROUP_WIDTH) @ w_out


def conformer_conv_mixer(h, w_in, b_in, dw_w, dw_b, ln_g, ln_b, w_out, b_out):
    u = h @ w_in + b_in
    a, gate = jnp.split(u, 2, axis=-1)
    u = a * jax.nn.sigmoid(gate)
    u = causal_depthwise_conv(u, dw_w, dw_b)
    u = jax.nn.silu(layer_norm(u, ln_g, ln_b))
    return u @ w_out + b_out


def hgrn2_chunk_scan(q, k, v, log_f):
    B, S, H, K = q.shape
    V = v.shape[-1]
    C = HGRN_CHUNK
    n = S // C

    def chunks(t):
        return t.reshape(B, n, C, H, t.shape[-1]).transpose(1, 0, 3, 2, 4)

    qc, kc, vc = chunks(q), chunks(k), chunks(v)
    bc = jnp.cumsum(chunks(log_f), axis=-2)
    tri = jnp.tril(jnp.ones((C, C), dtype=bool))[:, :, None]

    def step(state, xs):
        q_, k_, v_, b_ = xs
        o_inter = jnp.einsum('bhtk,bhkv->bhtv', q_ * jnp.exp(b_), state)
        diff = b_[:, :, :, None, :] - b_[:, :, None, :, :]
        decay = jnp.where(tri, jnp.exp(jnp.where(tri, diff, 0.0)), 0.0)
        scores = jnp.einsum('bhtsk,bhsk->bhts', q_[:, :, :, None, :] * decay, k_)
        o_intra = jnp.einsum('bhts,bhsv->bhtv', scores, v_)
        b_last = b_[:, :, -1, :]
        k_dec = k_ * jnp.exp(b_last[:, :, None, :] - b_)
        state = state * jnp.exp(b_last)[..., None] + jnp.einsum('bhsk,bhsv->bhkv', k_dec, v_)
        return state, o_inter + o_intra

    state0 = jnp.zeros((B, H, K, V), jnp.float32)
    _, o = lax.scan(step, state0, (qc, kc, vc, bc))
    return o.transpose(1, 0, 3, 2, 4).reshape(B, S, H, V)


def hgrn2_mixer(h, w_in, lower_bound, norm_gain, w_out):
    B, S, _ = h.shape
    q, f, i, g = jnp.split(h @ w_in, [HGRN_KEY_DIM, 2 * HGRN_KEY_DIM,
                                      2 * HGRN_KEY_DIM + HGRN_VALUE_DIM], axis=-1)
    q = jax.nn.silu(q.astype(jnp.float32))
    lb = lower_bound.astype(jnp.float32)
    log_f = jnp.logaddexp(jnp.log(lb), jnp.log1p(-lb) + jax.nn.log_sigmoid(f.astype(jnp.float32)))
    k = -jnp.expm1(log_f)
    heads = lambda t, dim: t.reshape(B, S, HGRN_HEADS, dim)
    o = hgrn2_chunk_scan(heads(q, HGRN_EXPAND), heads(k, HGRN_EXPAND),
                         heads(i.astype(jnp.float32), HGRN_HEAD_V), heads(log_f, HGRN_EXPAND))
    o = rms_norm(o, norm_gain.reshape(HGRN_HEADS, HGRN_HEAD_V)) * \
        jax.nn.silu(heads(g.astype(jnp.float32), HGRN_HEAD_V))
    return o.reshape(B, S, HGRN_VALUE_DIM).astype(h.dtype) @ w_out


def conv_ffn(h, w_up, conv_w, conv_b, w_down):
    u = causal_depthwise_conv(h @ w_up, conv_w, conv_b)
    gate, up = jnp.split(u, 2, axis=-1)
    return (jax.nn.silu(gate) * up) @ w_down


def setup_inputs(seed: int = 0) -> dict:
    key = jax.random.key(seed)
    ks = iter(jax.random.split(key, 32))
    D = D_MODEL
    F2 = 2 * FFN_HIDDEN

    def nrm(shape, scale):
        return scale * jax.random.normal(next(ks), shape, jnp.float32)

    def gain(shape):
        return 1.0 + nrm(shape, 0.02)

    return {
        "x": nrm((BATCH, SEQ, D), 1.0),
        "mixer_norm": gain((DEPTH, D)),
        "ffn_norm": gain((DEPTH, D)),
        "attn_w_in": nrm((N_ATTN_LAYERS, D, ATTN_IN_WIDTH), D ** -0.5),
        "attn_q_gain": gain((N_ATTN_LAYERS, ATTN_N_GROUPS, ATTN_HEAD_DIM)),
        "attn_k_gain": gain((N_ATTN_LAYERS, ATTN_N_GROUPS, ATTN_HEAD_DIM)),
        "attn_w_out": nrm((N_ATTN_LAYERS, ATTN_GROUP_WIDTH, D), ATTN_GROUP_WIDTH ** -0.5),
        "conv_w_in": nrm((N_CONV_LAYERS, D, 2 * CONV_CHANNELS), D ** -0.5),
        "conv_b_in": nrm((N_CONV_LAYERS, 2 * CONV_CHANNELS), 0.02),
        "conv_dw_w": nrm((N_CONV_LAYERS, CONV_WIDTH, CONV_CHANNELS), CONV_WIDTH ** -0.5),
        "conv_dw_b": nrm((N_CONV_LAYERS, CONV_CHANNELS), 0.02),
        "conv_ln_g": gain((N_CONV_LAYERS, CONV_CHANNELS)),
        "conv_ln_b": nrm((N_CONV_LAYERS, CONV_CHANNELS), 0.02),
        "conv_w_out": nrm((N_CONV_LAYERS, CONV_CHANNELS, D), CONV_CHANNELS ** -0.5),
        "conv_b_out": nrm((N_CONV_LAYERS, D), 0.02),
        "hgrn_w_in": nrm((N_HGRN_LAYERS, D, 2 * HGRN_KEY_DIM + 2 * HGRN_VALUE_DIM), D ** -0.5),
        "hgrn_lb_logits": nrm((DEPTH, HGRN_KEY_DIM), 1.0),
        "hgrn_norm_g": gain((N_HGRN_LAYERS, HGRN_VALUE_DIM)),
        "hgrn_w_out": nrm((N_HGRN_LAYERS, HGRN_VALUE_DIM, D), HGRN_VALUE_DIM ** -0.5),
        "ffn_w_up": nrm((DEPTH, D, F2), D ** -0.5),
        "ffn_conv_w": nrm((DEPTH, FFN_CONV_WIDTH, F2), FFN_CONV_WIDTH ** -0.5),
        "ffn_conv_b": nrm((DEPTH, F2), 0.02),
        "ffn_w_down": nrm((DEPTH, FFN_HIDDEN, D), FFN_HIDDEN ** -0.5),
    }


def reference(x, mixer_norm, ffn_norm, attn_w_in, attn_q_gain, attn_k_gain, attn_w_out,
              conv_w_in, conv_b_in, conv_dw_w, conv_dw_b, conv_ln_g, conv_ln_b, conv_w_out,
              conv_b_out, hgrn_w_in, hgrn_lb_logits, hgrn_norm_g, hgrn_w_out,
              ffn_w_up, ffn_conv_w, ffn_conv_b, ffn_w_down):
    lb_cum = jnp.cumsum(jax.nn.softmax(hgrn_lb_logits.astype(jnp.float32), axis=0), axis=0)
    lower_bounds = lb_cum - lb_cum[0]
    for layer in range(DEPTH):
        kind = layer % N_MIXERS
        j = layer // N_MIXERS
        h = rms_norm(x, mixer_norm[layer])
        if kind == 0:
            mix = dilated_attention_mixer(h, attn_w_in[j], attn_q_gain[j], attn_k_gain[j],
                                          attn_w_out[j])
        elif kind == 1:
            mix = conformer_conv_mixer(h, conv_w_in[j], conv_b_in[j], conv_dw_w[j], conv_dw_b[j],
                                       conv_ln_g[j], conv_ln_b[j], conv_w_out[j], conv_b_out[j])
        else:
            mix = hgrn2_mixer(h, hgrn_w_in[j], lower_bounds[layer], hgrn_norm_g[j], hgrn_w_out[j])
        x = x + mix
        x = x + conv_ffn(rms_norm(x, ffn_norm[layer]), ffn_w_up[layer], ffn_conv_w[layer],
                         ffn_conv_b[layer], ffn_w_down[layer])
    return x
```

```python
from contextlib import ExitStack
import numpy as np
import concourse.bass as bass
import concourse.mybir as mybir
from concourse.bass_utils import run_bass_kernel_spmd

F32 = mybir.dt.float32
BF16 = mybir.dt.bfloat16
AF = mybir.ActivationFunctionType
ALU = mybir.AluOpType
AX = mybir.AxisListType

D = 1024
FH = 2816
F2 = 5632
EPS = 1e-6
NCORES = 8


class Prog:
    STREAMS = ("pe", "act", "dve", "pool", "sp")

    def __init__(self, nc, es):
        self.nc = nc
        self.es = es
        self.ops = []
        self.outkeys = []

    def sb(self, name, shape, dt):
        return self.es.enter_context(self.nc.sbuf_tensor("sb_" + name, list(shape), dt))

    def ps(self, name, shape, dt=F32):
        return self.es.enter_context(self.nc.psum_tensor("pp_" + name, list(shape), dt))

    def op(self, stream, fn, reads=(), writes=()):
        self.ops.append(dict(stream=stream, fn=fn, reads=tuple(reads), writes=tuple(writes),
                             dma=False, dkey=None))

    def dma(self, stream, out, in_, reads=(), writes=(), dkey=None, **kw):
        def fn(e, out=out, in_=in_, kw=kw):
            return e.dma_start(out=out, in_=in_, **kw)
        if dkey is None:
            dkey = ("dq", stream)
        self.ops.append(dict(stream=stream, fn=fn, reads=tuple(reads), writes=tuple(writes),
                             dma=True, dkey=dkey))

    def finish(self, keys):
        self.ops.append(dict(stream="sp", fn=None, reads=tuple(keys), writes=(), dma=False, dkey=None))

    def emit(self):
        nc = self.nc
        ops = self.ops
        ordn = {}
        for i, o in enumerate(ops):
            sid = ("dma", o["dkey"]) if o["dma"] else ("eng", o["stream"])
            o["sid"] = sid
            o["ord"] = ordn.get(sid, 0)
            ordn[sid] = o["ord"] + 1
        last_w = {}
        readers = {}
        seen = {s: {} for s in self.STREAMS}
        signal = set()
        for i, o in enumerate(ops):
            deps = set()
            for k in o["reads"]:
                if k in last_w:
                    deps.add(last_w[k])
            for k in o["writes"]:
                if k in last_w:
                    deps.add(last_w[k])
                for r in readers.get(k, ()):
                    deps.add(r)
            waits = {}
            for j in deps:
                p = ops[j]
                if p["fn"] is None:
                    continue
                if (not p["dma"]) and (not o["dma"]) and p["stream"] == o["stream"] and o["stream"] == "pe":
                    continue
                sid, od = p["sid"], p["ord"]
                if seen[o["stream"]].get(sid, -1) >= od:
                    continue
                if waits.get(sid, -1) < od:
                    waits[sid] = od
            for sid, od in waits.items():
                seen[o["stream"]][sid] = od
                signal.add((sid, od))
            o["waits"] = waits
            for k in o["reads"]:
                readers.setdefault(k, []).append(i)
            for k in o["writes"]:
                last_w[k] = i
                readers[k] = []
        by_sid = {}
        for sid, od in signal:
            by_sid.setdefault(sid, []).append(od)
        rank = {}
        for sid, lst in by_sid.items():
            lst.sort()
            for r, od in enumerate(lst):
                rank[(sid, od)] = r + 1
        sems = {}
        for n, sid in enumerate(sorted(by_sid.keys(), key=str)):
            sems[sid] = self.es.enter_context(nc.semaphore("s%d" % n))
        self.n_sems = len(sems)
        self.sem_max = {str(sid): len(lst) * (16 if sid[0] == "dma" else 1) for sid, lst in by_sid.items()}
        import os
        if os.environ.get("PROG_DEBUG"):
            print("PROG sems", self.n_sems, "max values", sorted(self.sem_max.items(), key=lambda kv: -kv[1])[:8])
        streams = {s: [o for o in ops if o["stream"] == s] for s in self.STREAMS}

        def run(e, sname):
            for o in streams[sname]:
                for sid, od in o["waits"].items():
                    mult = 16 if sid[0] == "dma" else 1
                    e.wait_ge(sems[sid], rank[(sid, od)] * mult)
                if o["fn"] is None:
                    continue
                ins = o["fn"](e)
                key = (o["sid"], o["ord"])
                if key in signal:
                    ins.then_inc(sems[o["sid"]], 16 if o["dma"] else 1)

        with nc.Block() as block:
            @block.tensor
            def _(e):
                run(e, "pe")

            @block.scalar
            def _(e):
                run(e, "act")

            @block.vector
            def _(e):
                run(e, "dve")

            @block.gpsimd
            def _(e):
                run(e, "pool")

            @block.sync
            def _(e):
                run(e, "sp")


class Consts:
    def __init__(self, P, ident_dram):
        self.ident = P.sb("ident", [128, 128], BF16)
        stg = P.sb("ident_stg", [128, 128], F32)
        P.dma("sp", stg[:], ident_dram, writes=["ident_stg"], dkey="ident_stg")
        P.op("dve", lambda e: e.tensor_copy(out=self.ident[:], in_=stg[:]),
             reads=["ident_stg"], writes=["ident"])


def load_cast_weight(P, name, w_sb, w_dram_rows, ncols, stg, gain_col=None, col_chunk=704, eng_cycle=("pool", "dve", "act")):
    cnt = 0
    for r, src in enumerate(w_dram_rows):
        for c0 in range(0, ncols, col_chunk):
            cw = min(col_chunk, ncols - c0)
            st, skey = stg[cnt % len(stg)]
            P.dma("sp", st[:, 0:cw], src[:, c0:c0 + cw], writes=[skey], dkey=skey)
            eng = eng_cycle[cnt % len(eng_cycle)]
            dst = w_sb[:, r, c0:c0 + cw]
            srcap = st[:, 0:cw]
            if gain_col is None:
                if eng == "act":
                    fn = lambda e, dst=dst, srcap=srcap: e.copy(out=dst, in_=srcap)
                else:
                    fn = lambda e, dst=dst, srcap=srcap: e.tensor_copy(out=dst, in_=srcap)
            else:
                g = gain_col(r)
                if eng == "act":
                    fn = lambda e, dst=dst, srcap=srcap, g=g: e.activation(out=dst, in_=srcap, func=AF.Identity, scale=g)
                else:
                    fn = lambda e, dst=dst, srcap=srcap, g=g: e.tensor_scalar(out=dst, in0=srcap, scalar1=g, scalar2=None, op0=ALU.mult)
            P.op(eng, fn, reads=[skey, "gains"], writes=[(name, r, c0)])
            cnt += 1
    return [(name, r, c0) for r in range(len(w_dram_rows)) for c0 in range(0, ncols, col_chunk)]


class NormT:
    def __init__(self, P, consts, tag, nbuf=2):
        self.P, self.c, self.tag = P, consts, tag
        self.hn = [P.sb("%s_hn%d" % (tag, i), [128, D], BF16) for i in range(nbuf)]
        self.st = [P.sb("%s_st%d" % (tag, i), [128, 4], F32) for i in range(nbuf)]
        self.psT = P.ps("%s_psT" % tag, [128, 8, 128], BF16)
        self.eps = P.sb("%s_eps" % tag, [128, 1], F32)
        P.op("pool", lambda e: e.memset(self.eps[:], EPS), writes=[tag + "_eps"])
        self.n = 0

    def stats_and_scale(self, x_ap, xkey):
        P, tag = self.P, self.tag
        i = self.n % len(self.hn)
        hn, st = self.hn[i], self.st[i]
        hk, sk = (tag, "hn", i), (tag, "st", i)
        P.op("act", lambda e: e.activation(out=hn[:], in_=x_ap, func=AF.Square, accum_out=st[:, 0:1]),
             reads=[xkey], writes=[hk, sk])
        P.op("act", lambda e: e.activation(out=st[:, 1:2], in_=st[:, 0:1], func=AF.Sqrt, bias=self.eps[:], scale=1.0 / D),
             reads=[sk, tag + "_eps"], writes=[sk])
        P.op("dve", lambda e: e.reciprocal(out=st[:, 2:3], in_=st[:, 1:2]), reads=[sk], writes=[sk])
        P.op("dve", lambda e: e.tensor_scalar(out=hn[:], in0=x_ap, scalar1=st[:, 2:3], scalar2=None, op0=ALU.mult),
             reads=[xkey, sk, hk], writes=[hk])
        self.n += 1
        return i

    def transpose_to(self, i, hT_ap3, hTkey, evac_eng="act"):
        P, tag = self.P, self.tag
        hn = self.hn[i]
        hk = (tag, "hn", i)
        pk = (tag, "psT")
        for k in range(8):
            P.op("pe", lambda e, k=k: e.transpose(out=self.psT[:, k, :], in_=hn[:, k * 128:(k + 1) * 128], identity=self.c.ident[:]),
                 reads=[hk, "ident"], writes=[pk])
        if evac_eng == "act":
            P.op("act", lambda e: e.copy(out=hT_ap3, in_=self.psT[:]), reads=[pk], writes=[hTkey])
        else:
            P.op("dve", lambda e: e.tensor_copy(out=hT_ap3, in_=self.psT[:]), reads=[pk], writes=[hTkey])


def build_ffn(NT):
    assert NT % 512 == 0
    nc = bass.Bass("TRN2", target_bir_lowering=False)
    xin = nc.dram_tensor("xin", [NT + 128, D], F32, kind="ExternalInput").ap()
    w_up = nc.dram_tensor("w_up", [D, F2], F32, kind="ExternalInput").ap()
    w_dn = nc.dram_tensor("w_dn", [FH, D], F32, kind="ExternalInput").ap()
    gain_d = nc.dram_tensor("gain", [128, 8], F32, kind="ExternalInput").ap()
    cw_d = nc.dram_tensor("cw", [128, 44 * 3], F32, kind="ExternalInput").ap()
    cb_d = nc.dram_tensor("cb", [128, 44], F32, kind="ExternalInput").ap()
    ident_d = nc.dram_tensor("ident", [128, 128], F32, kind="ExternalInput").ap()
    out = nc.dram_tensor("out", [NT, D], F32, kind="ExternalOutput").ap()
    with ExitStack() as es:
        P = Prog(nc, es)
        emit_ffn(P, NT, xin, w_up, w_dn, gain_d, cw_d, cb_d, ident_d, out)
        P.finish(P.outkeys)
        P.emit()
    return nc


def emit_ffn(P, NT, xin, w_up, w_dn, gain_d, cw_d, cb_d, ident_d, out):
    consts = Consts(P, ident_d)
    gain = P.sb("gain", [128, 8], F32)
    cw = P.sb("cw", [128, 44 * 3], F32)
    cb = P.sb("cb", [128, 44], F32)
    P.dma("sp", gain[:], gain_d, writes=["gains"], dkey="gains")
    P.dma("sp", cw[:], cw_d, writes=["cw"], dkey="cw")
    P.dma("sp", cb[:], cb_d, writes=["cb"], dkey="cb")
    wup = P.sb("wup", [128, 8, F2], BF16)
    wdn = P.sb("wdn", [128, 22, D], BF16)
    stg = [(P.sb("wstg%d" % i, [128, 512], F32), "wstg%d" % i) for i in range(2)]
    up_keys = load_cast_weight(P, "wup", wup, [w_up[k * 128:(k + 1) * 128, :] for k in range(8)], F2, stg,
                               gain_col=lambda r: gain[:, r:r + 1], col_chunk=512)
    dn_keys = load_cast_weight(P, "wdn", wdn, [w_dn[j * 128:(j + 1) * 128, :] for j in range(22)], D, stg,
                               col_chunk=512)

    norm = NormT(P, consts, "n", nbuf=4)
    hT = P.sb("hT", [128, 8, 512], BF16)
    actT = P.sb("actT", [128, 22, 512], BF16)
    carry2 = [P.sb("carry%d" % i, [128, 44, 2], F32) for i in range(2)]
    x1 = [P.sb("x1_%d" % i, [128, D], F32) for i in range(2)]
    xr = [P.sb("xr_%d" % i, [128, D], F32) for i in range(2)]
    yg = [P.sb("yg%d" % i, [128, 512], F32) for i in range(2)]
    yu = [P.sb("yu%d" % i, [128, 512], F32) for i in range(2)]
    gs = [P.sb("gs%d" % i, [128, 512], BF16) for i in range(2)]
    ps_u = [P.ps("ps_u%d" % i, [128, 512]) for i in range(3)]
    ps_d = [P.ps("ps_d%d" % i, [128, 512]) for i in range(2)]
    cnt = dict(x1=0, xr=0, u=0, d=0, y=0)

    def s1a(row0, nsub):
        res = []
        for s in range(nsub):
            i = cnt["x1"] % 2
            cnt["x1"] += 1
            xk = ("x1", i)
            P.dma("sp", x1[i][:], xin[row0 + s * 128: row0 + (s + 1) * 128, :], writes=[xk], dkey=xk)
            res.append(norm.stats_and_scale(x1[i][:], xk))
        return res

    def s1b(idx, col0):
        for s, i in enumerate(idx):
            norm.transpose_to(i, hT[:, :, col0 + s * 128: col0 + (s + 1) * 128], ("hT", col0 // 128 + s))

    def up_chunk(c, ncol, hkeys):
        b = cnt["u"] % 3
        cnt["u"] += 1
        pk = ("ps_u", b)
        for k in range(8):
            P.op("pe", lambda e, k=k, b=b: e.matmul(out=ps_u[b][:, 0:ncol], lhsT=wup[:, k, c * 128:(c + 1) * 128],
                                                     rhs=hT[:, k, 0:ncol], start=(k == 0), stop=(k == 7)),
                 reads=list(hkeys) + [("wup", k, (c * 128 // 512) * 512)],
                 writes=[pk])
        return ps_u[b], pk

    idx = s1a(0, 1)
    s1b(idx, 0)
    for c in range(44):
        pu, pk = up_chunk(c, 128, [("hT", 0)])
        P.op("act", lambda e, pu=pu, c=c: e.copy(out=carry2[0][:, c, :], in_=pu[:, 126:128]), reads=[pk], writes=[("carry", 0, c)])

    ntiles = NT // 512
    idx = s1a(128, 4)
    s1b(idx, 0)
    for t in range(ntiles):
        hkeys = [("hT", s) for s in range(4)]
        carry, carryn = carry2[t % 2], carry2[(t + 1) % 2]
        for j in range(22):
            ybufs = {}
            for which, c in (("g", j), ("u", j + 22)):
                pu, pk = up_chunk(c, 512, hkeys)
                yi = cnt["y"] % 2
                y = (yg if which == "g" else yu)[yi]
                yk = ("y" + which, yi)
                ck = ("carry", t % 2, c)
                ckn = ("carry", (t + 1) % 2, c)
                w0, w1, w2 = (cw[:, c * 3 + q: c * 3 + q + 1] for q in range(3))
                P.op("act", lambda e, y=y, pu=pu, w2=w2, c=c: e.activation(out=y[:], in_=pu[:], func=AF.Identity, bias=cb[:, c:c + 1], scale=w2),
                     reads=[pk, "cw", "cb"], writes=[yk])
                P.op("dve", lambda e, y=y, pu=pu, w1=w1: e.scalar_tensor_tensor(out=y[:, 1:512], in0=pu[:, 0:511], scalar=w1, in1=y[:, 1:512], op0=ALU.mult, op1=ALU.add),
                     reads=[pk, yk, "cw"], writes=[yk])
                P.op("dve", lambda e, y=y, pu=pu, w0=w0: e.scalar_tensor_tensor(out=y[:, 2:512], in0=pu[:, 0:510], scalar=w0, in1=y[:, 2:512], op0=ALU.mult, op1=ALU.add),
                     reads=[pk, yk, "cw"], writes=[yk])
                P.op("dve", lambda e, y=y, w0=w0, c=c, carry=carry: e.scalar_tensor_tensor(out=y[:, 0:2], in0=carry[:, c, :], scalar=w0, in1=y[:, 0:2], op0=ALU.mult, op1=ALU.add),
                     reads=[ck, yk, "cw"], writes=[yk])
                P.op("dve", lambda e, y=y, w1=w1, c=c, carry=carry: e.scalar_tensor_tensor(out=y[:, 0:1], in0=carry[:, c, 1:2], scalar=w1, in1=y[:, 0:1], op0=ALU.mult, op1=ALU.add),
                     reads=[ck, yk, "cw"], writes=[yk])
                P.op("act", lambda e, pu=pu, c=c, carryn=carryn: e.copy(out=carryn[:, c, :], in_=pu[:, 510:512]), reads=[pk], writes=[ckn])
                ybufs[which] = (y, yk)
            cnt["y"] += 1
            (yG, ygk), (yU, yuk) = ybufs["g"], ybufs["u"]
            g = gs[j % 2]
            gk = ("gs", j % 2)
            P.op("act", lambda e, g=g, yG=yG: e.activation(out=g[:], in_=yG[:], func=AF.Silu), reads=[ygk], writes=[gk])
            P.op("pool", lambda e, g=g, yU=yU, j=j: e.tensor_tensor(out=actT[:, j, :], in0=g[:], in1=yU[:], op=ALU.mult),
                 reads=[gk, yuk], writes=[("actT", j)])
        nidx = s1a(128 + (t + 1) * 512, 4) if t + 1 < ntiles else None
        for s in range(4):
            i = cnt["xr"] % 2
            cnt["xr"] += 1
            xk = ("xr", i)
            r0 = t * 512 + s * 128
            P.dma("sp", xr[i][:], xin[128 + r0: 128 + r0 + 128, :], writes=[xk], dkey=xk)
            for half in range(2):
                b = cnt["d"] % 2
                cnt["d"] += 1
                pk = ("ps_d", b)
                for j in range(22):
                    P.op("pe", lambda e, j=j, b=b, s=s, half=half: e.matmul(out=ps_d[b][:], lhsT=actT[:, j, s * 128:(s + 1) * 128],
                                                                          rhs=wdn[:, j, half * 512:(half + 1) * 512], start=(j == 0), stop=(j == 21)),
                         reads=[("actT", j), ("wdn", j, half * 512)], writes=[pk])
                P.op("dve", lambda e, b=b, i=i, half=half: e.tensor_tensor(out=xr[i][:, half * 512:(half + 1) * 512], in0=ps_d[b][:],
                                                                         in1=xr[i][:, half * 512:(half + 1) * 512], op=ALU.add),
                     reads=[pk, xk], writes=[xk])
            P.dma("sp", out[r0:r0 + 128, :], xr[i][:], reads=[xk], writes=[("OUT", r0)], dkey=xk)
            P.outkeys.append(("OUT", r0))
        if nidx is not None:
            s1b(nidx, 0)


def _ident():
    return np.eye(128, dtype=np.float32)


def ffn_host_inputs(g, w_up, conv_w, conv_b, w_dn):
    return dict(
        w_up=np.ascontiguousarray(w_up), w_dn=np.ascontiguousarray(w_dn),
        gain=np.ascontiguousarray(g.reshape(8, 128).T),
        cw=np.ascontiguousarray(conv_w.reshape(3, 44, 128).transpose(2, 1, 0).reshape(128, 132)),
        cb=np.ascontiguousarray(conv_b.reshape(44, 128).T),
        ident=_ident())


def build_conf(NT):
    assert NT % 512 == 0
    nc = bass.Bass("TRN2", target_bir_lowering=False)
    dt = lambda n, s: nc.dram_tensor(n, s, F32, kind="ExternalInput").ap()
    a = dict(
        xin=dt("xin", [NT + 128, D]), w_in=dt("w_in", [D, 2 * D]), w_out=dt("w_out", [D, D]),
        gain=dt("gain", [128, 8]), b_in=dt("b_in", [128, 16]), dw=dt("dw", [128, 8 * 31]),
        dwb=dt("dwb", [D]), ln_g=dt("ln_g", [128, 8]), ln_b=dt("ln_b", [128, 8]), bout=dt("bout", [D]),
        hv=dt("hv", [128, 1]), ident=dt("ident", [128, 128]))
    out = nc.dram_tensor("out", [NT, D], F32, kind="ExternalOutput").ap()
    with ExitStack() as es:
        P = Prog(nc, es)
        emit_conf(P, NT, a, out)
        P.finish(P.outkeys)
        P.emit()
    return nc


def emit_conf(P, NT, a, out):
    xin = a["xin"]
    consts = Consts(P, a["ident"])
    identf = P.sb("identf", [128, 128], F32)
    P.dma("sp", identf[:], a["ident"], writes=["identf"], dkey="identf")
    small = {}
    for nm, shp in (("gain", [128, 8]), ("b_in", [128, 16]), ("dw", [128, 248]), ("ln_g", [128, 8]), ("ln_b", [128, 8]), ("hv", [128, 1])):
        t = P.sb("c_" + nm, shp, F32)
        P.dma("sp", t[:], a[nm], writes=["gains" if nm == "gain" else "c_" + nm], dkey="c_" + nm)
        small[nm] = t
    dwb_bc = P.sb("dwb_bc", [128, D], F32)
    bout_bc = P.sb("bout_bc", [128, D], F32)
    P.dma("sp", dwb_bc[:], a["dwb"].partition_broadcast(128), writes=["dwb_bc"], dkey="dwb_bc")
    P.dma("sp", bout_bc[:], a["bout"].partition_broadcast(128), writes=["bout_bc"], dkey="bout_bc")
    win = P.sb("win", [128, 8, 2 * D], BF16)
    wout = P.sb("wout", [128, 8, D], BF16)
    stg = [(P.sb("wstg%d" % i, [128, 512], F32), "wstg%d" % i) for i in range(2)]
    load_cast_weight(P, "win", win, [a["w_in"][k * 128:(k + 1) * 128, :] for k in range(8)], 2 * D, stg,
                     gain_col=lambda r: small["gain"][:, r:r + 1], col_chunk=512)
    load_cast_weight(P, "wout", wout, [a["w_out"][k * 128:(k + 1) * 128, :] for k in range(8)], D, stg, col_chunk=512)
    dg = P.sb("dg", [128, 248, 128], BF16)
    for i in range(248):
        eng = ("pool", "dve")[i % 2]
        P.op(eng, lambda e, i=i: e.tensor_scalar(out=dg[:, i, :], in0=identf[:], scalar1=small["dw"][:, i:i + 1], scalar2=None, op0=ALU.mult),
             reads=["identf", "c_dw"], writes=[("dg", i)])
    eps_t = P.sb("c_eps", [128, 1], F32)
    P.op("pool", lambda e: e.memset(eps_t[:], EPS), writes=["c_eps"])

    norm = NormT(P, consts, "n", nbuf=4)
    hT = P.sb("hT", [128, 8, 512], BF16)
    vbuf = [P.sb("vbuf%d" % i, [128, 8, 542], BF16) for i in range(2)]
    zT = P.sb("zT", [128, 8, 512], BF16)
    sg = [P.sb("sg%d" % i, [128, 512], F32) for i in range(2)]
    yb = [P.sb("yb%d" % i, [128, D], F32) for i in range(2)]
    lnst = [P.sb("lnst%d" % i, [128, 16], F32) for i in range(2)]
    x1 = [P.sb("x1_%d" % i, [128, D], F32) for i in range(2)]
    xr = [P.sb("xr_%d" % i, [128, D], F32) for i in range(2)]
    ps_a = P.ps("ps_a", [128, 512])
    ps_g = P.ps("ps_g", [128, 512])
    psC = P.ps("psC", [128, D])
    psZ = P.ps("psZ", [128, 8, 128])
    psO = P.ps("psO", [128, 512])
    cnt = dict(x1=0, xr=0, y=0, sg=0)

    def s1(row0, nsub, col0=0):
        for s in range(nsub):
            i = cnt["x1"] % 2
            cnt["x1"] += 1
            xk = ("x1", i)
            P.dma("sp", x1[i][:], xin[row0 + s * 128: row0 + (s + 1) * 128, :], writes=[xk], dkey=xk)
            hi = norm.stats_and_scale(x1[i][:], xk)
            norm.transpose_to(hi, hT[:, :, col0 + s * 128: col0 + (s + 1) * 128], ("hT", col0 // 128 + s))

    def inproj(vb, vkeyf, ncol, dst0, hkeys, mask_hv=False):
        for c in range(8):
            for which, ps, cc in (("a", ps_a, c), ("g", ps_g, c + 8)):
                for k in range(8):
                    P.op("pe", lambda e, k=k, ps=ps, cc=cc: e.matmul(out=ps[:, 0:ncol], lhsT=win[:, k, cc * 128:(cc + 1) * 128], rhs=hT[:, k, 0:ncol],
                                                                    start=(k == 0), stop=(k == 7)),
                         reads=list(hkeys) + [("win", k, (cc * 128 // 512) * 512)], writes=["ps_" + which])
            sgi = cnt["sg"] % 2
            cnt["sg"] += 1
            sgt, sgk = sg[sgi], ("sg", sgi)
            P.op("act", lambda e, c=c, sgt=sgt: e.activation(out=sgt[:, 0:ncol], in_=ps_g[:, 0:ncol], func=AF.Sigmoid, bias=small["b_in"][:, c + 8:c + 9], scale=1.0),
                 reads=["ps_g", "c_b_in"], writes=[sgk])
            P.op("dve", lambda e, c=c, sgt=sgt: e.scalar_tensor_tensor(out=vb[:, c, dst0:dst0 + ncol], in0=ps_a[:, 0:ncol], scalar=small["b_in"][:, c:c + 1],
                                                                    in1=sgt[:, 0:ncol], op0=ALU.add, op1=ALU.mult),
                 reads=["ps_a", sgk, "c_b_in"], writes=[vkeyf(c)])
            if mask_hv:
                P.op("dve", lambda e, c=c: e.tensor_scalar(out=vb[:, c, dst0:dst0 + ncol], in0=vb[:, c, dst0:dst0 + ncol], scalar1=small["hv"][:, 0:1], scalar2=None, op0=ALU.mult),
                     reads=[vkeyf(c), "c_hv"], writes=[vkeyf(c)])

    s1(0, 1)
    inproj(vbuf[1], lambda c: ("vbuf", 1, c), 128, 542 - 128, [("hT", 0)], mask_hv=True)
    ntiles = NT // 512
    for t in range(ntiles):
        cur, prv = t % 2, (t + 1) % 2
        s1(128 + t * 512, 4)
        hkeys = [("hT", s) for s in range(4)]
        for c in range(8):
            P.op("pool", lambda e, c=c, cur=cur, prv=prv: e.tensor_copy(out=vbuf[cur][:, c, 0:30], in_=vbuf[prv][:, c, 512:542]),
                 reads=[("vbuf", prv, c)], writes=[("vbufh", cur, c)])
        inproj(vbuf[cur], lambda c, cur=cur: ("vbuf", cur, c), 512, 30, hkeys)
        for s in range(4):
            for c in range(8):
                for j in range(31):
                    P.op("pe", lambda e, c=c, j=j, s=s, cur=cur: e.matmul(out=psC[:, c * 128:(c + 1) * 128], lhsT=vbuf[cur][:, c, s * 128 + j: s * 128 + j + 128],
                                                                        rhs=dg[:, c * 31 + j, :], start=(j == 0), stop=(j == 30)),
                         reads=[("vbuf", cur, c), ("vbufh", cur, c), ("dg", c * 31 + j)], writes=["psC"])
            yi = cnt["y"] % 2
            cnt["y"] += 1
            y, yk, st, sk = yb[yi], ("yb", yi), lnst[yi], ("lnst", yi)
            for half in range(2):
                sl = slice(half * 512, (half + 1) * 512)
                P.op("dve", lambda e, y=y, sl=sl: e.tensor_tensor(out=y[:, sl], in0=psC[:, sl], in1=dwb_bc[:, sl], op=ALU.add),
                     reads=["psC", "dwb_bc"], writes=[yk])
            for half in range(2):
                sl = slice(half * 512, (half + 1) * 512)
                P.op("dve", lambda e, y=y, st=st, sl=sl, half=half: e.bn_stats(out=st[:, half * 6:(half + 1) * 6], in_=y[:, sl]), reads=[yk], writes=[sk])
            P.op("dve", lambda e, st=st: e.bn_aggr(out=st[:, 12:14], in_=st[:, 0:12]), reads=[sk], writes=[sk])
            P.op("act", lambda e, st=st: e.activation(out=st[:, 14:15], in_=st[:, 13:14], func=AF.Sqrt, bias=eps_t[:], scale=1.0), reads=[sk, "c_eps"], writes=[sk])
            P.op("dve", lambda e, st=st: e.reciprocal(out=st[:, 15:16], in_=st[:, 14:15]), reads=[sk], writes=[sk])
            P.op("dve", lambda e, y=y, st=st: e.tensor_scalar(out=y[:], in0=y[:], scalar1=st[:, 12:13], scalar2=st[:, 15:16], op0=ALU.subtract, op1=ALU.mult),
                 reads=[yk, sk], writes=[yk])
            for c in range(8):
                P.op("pe", lambda e, c=c, y=y: e.transpose(out=psZ[:, c, :], in_=y[:, c * 128:(c + 1) * 128], identity=identf[:]),
                     reads=[yk, "identf"], writes=[("psZ", c)])
                P.op("act", lambda e, c=c, s=s: e.activation(out=zT[:, c, s * 128:(s + 1) * 128], in_=psZ[:, c, :], func=AF.Silu,
                                                             bias=small["ln_b"][:, c:c + 1], scale=small["ln_g"][:, c:c + 1]),
                     reads=[("psZ", c), "c_ln_g", "c_ln_b"], writes=[("zT", s)])
            i = cnt["xr"] % 2
            cnt["xr"] += 1
            xk = ("xr", i)
            r0 = t * 512 + s * 128
            P.dma("sp", xr[i][:], xin[128 + r0: 128 + r0 + 128, :], writes=[xk], dkey=xk)
            P.op("pool", lambda e, i=i: e.tensor_tensor(out=xr[i][:], in0=xr[i][:], in1=bout_bc[:], op=ALU.add), reads=[xk, "bout_bc"], writes=[xk])
            for half in range(2):
                for c in range(8):
                    P.op("pe", lambda e, c=c, s=s, half=half: e.matmul(out=psO[:], lhsT=zT[:, c, s * 128:(s + 1) * 128], rhs=wout[:, c, half * 512:(half + 1) * 512],
                                                                     start=(c == 0), stop=(c == 7)),
                         reads=[("zT", s), ("wout", c, half * 512)], writes=["psO"])
                P.op("dve", lambda e, i=i, half=half: e.tensor_tensor(out=xr[i][:, half * 512:(half + 1) * 512], in0=psO[:], in1=xr[i][:, half * 512:(half + 1) * 512], op=ALU.add),
                     reads=["psO", xk], writes=[xk])
            P.dma("sp", out[r0:r0 + 128, :], xr[i][:], reads=[xk], writes=[("OUT", r0)], dkey=xk)
            P.outkeys.append(("OUT", r0))


def conf_host_inputs(g, w_in, b_in, dw_w, dw_b, ln_g, ln_b, w_out, b_out):
    c = np.ascontiguousarray
    return dict(w_in=c(w_in), w_out=c(w_out), gain=c(g.reshape(8, 128).T), b_in=c(b_in.reshape(16, 128).T),
                dw=c(dw_w.reshape(31, 8, 128).transpose(2, 1, 0).reshape(128, 248)), dwb=c(dw_b),
                ln_g=c(ln_g.reshape(8, 128).T), ln_b=c(ln_b.reshape(8, 128).T), bout=c(b_out), ident=_ident())


DIL = (1, 4, 16)
HALO = 2048
NDW = 8 * 65


def sls(start, n, step):
    return slice(start, start + (n - 1) * step + 1, step)


def build_attn(NT):
    assert NT % 2048 == 0
    nc = bass.Bass("TRN2", target_bir_lowering=False)
    dt = lambda n, s: nc.dram_tensor(n, s, F32, kind="ExternalInput").ap()
    a = dict(
        xin=dt("xin", [HALO + NT, D]), w_in=dt("w_in", [D, 4608]), w_out=dt("w_out", [512, D]),
        gain=dt("gain", [128, 8]), gq=dt("gq", [128, 3]), gk=dt("gk", [128, 3]), hv=dt("hv", [128, 1]),
        mask=dt("mask", [128, 512]), ident=dt("ident", [128, 128]))
    out = nc.dram_tensor("out", [NT, D], F32, kind="ExternalOutput").ap()
    nd = [nc.dram_tensor("nd%d" % g, [NT, NDW], F32, kind="Internal").ap() for g in range(3)]
    with ExitStack() as es:
        P = Prog(nc, es)
        emit_attn(P, NT, a, out, nd)
        P.finish(P.outkeys)
        P.emit()
    return nc


def emit_attn(P, NT, a, out, nd):
    xin = a["xin"]
    consts = Consts(P, a["ident"])
    small = {}
    for nm, shp in (("gain", [128, 8]), ("gq", [128, 3]), ("gk", [128, 3]), ("hv", [128, 1])):
        t = P.sb("a_" + nm, shp, F32)
        P.dma("sp", t[:], a[nm], writes=["gains" if nm == "gain" else "a_" + nm], dkey="a_" + nm)
        small[nm] = t
    qscale = P.sb("a_qscale", [128, 3], F32)
    P.op("dve", lambda e: e.scalar_tensor_tensor(out=qscale[:], in0=small["gq"][:], scalar=0.125, in1=small["gk"][:], op0=ALU.mult, op1=ALU.mult),
         reads=["a_gq", "a_gk"], writes=["a_qscale"])
    ones1 = P.sb("a_ones", [128, 1], F32)
    eps_t = P.sb("a_eps", [128, 1], F32)
    P.op("pool", lambda e: e.memset(ones1[:], 1.0), writes=["a_ones"])
    P.op("pool", lambda e: e.memset(eps_t[:], EPS), writes=["a_eps"])
    maskf = P.sb("a_maskf", [128, 512], F32)
    mask = P.sb("a_mask", [128, 512], BF16)
    P.dma("sp", maskf[:], a["mask"], writes=["a_maskf"], dkey="a_maskf")
    P.op("dve", lambda e: e.tensor_copy(out=mask[:], in_=maskf[:]), reads=["a_maskf"], writes=["a_mask"])

    win = P.sb("win", [128, 8, 4608], BF16)
    wo = P.sb("wo", [128, 4, D], BF16)
    stg = [(P.sb("wstg%d" % i, [128, 512], F32), "wstg%d" % i) for i in range(2)]
    load_cast_weight(P, "win", win, [a["w_in"][k * 128:(k + 1) * 128, :] for k in range(8)], 4608, stg,
                     gain_col=lambda r: small["gain"][:, r:r + 1], col_chunk=512)
    load_cast_weight(P, "wo", wo, [a["w_out"][k * 128:(k + 1) * 128, :] for k in range(4)], D, stg, col_chunk=512)

    norm = NormT(P, consts, "n", nbuf=2)
    hT = P.sb("hTs", [128, 8, 2048], BF16)
    x1 = P.sb("x1", [128, D], F32)
    xr = P.sb("xr", [128, D], F32)
    nslots = [d + 1 for d in DIL]
    KT = [[P.sb("KT%d_%d" % (g, i), [128, 4, 128], BF16) for i in range(nslots[g])] for g in range(3)]
    VA = [[P.sb("VA%d_%d" % (g, i), [128, 8, 65], BF16) for i in range(nslots[g])] for g in range(3)]
    slot = [list(range(d)) for d in DIL]
    free = [d for d in DIL]
    QT = P.sb("QT", [128, 4, 128], BF16)
    qn = P.sb("qn", [128, 512], BF16)
    sq = P.sb("sq", [128, 512], F32)
    nst = P.sb("nst", [128, 24], F32)
    pT = [P.sb("pT%d" % i, [128, 512], BF16) for i in range(2)]
    ot = [P.sb("ot%d" % i, [128, 8, 65], F32) for i in range(2)]
    cmb = [P.sb("cmb%d" % i, [128, 8, 65], F32) for i in range(3)]
    rD = P.sb("rD", [128, 8], F32)
    otm = P.sb("otm", [128, 512], BF16)
    oT = P.sb("oT", [128, 4, 128], BF16)
    ps_p = [P.ps("ps_p%d" % i, [128, 512]) for i in range(2)]
    ps_s = [P.ps("ps_s%d" % i, [128, 512]) for i in range(2)]
    ps_o = P.ps("ps_o", [128, 2, 512])
    psO = P.ps("psO", [128, 512])
    cnt = dict(p=0, s=0, pt=0, ot=0)
    nst_n = [0]

    def proj(st_cols, g, j):
        b = cnt["p"] % 2
        cnt["p"] += 1
        c0 = g * 1536 + j * 512
        for k in range(8):
            P.op("pe", lambda e, k=k, b=b, c0=c0: e.matmul(out=ps_p[b][:], lhsT=hT[:, k, st_cols], rhs=win[:, k, c0:c0 + 512], start=(k == 0), stop=(k == 7)),
                 reads=["hTs", ("win", k, c0)], writes=[("ps_p", b)])
        return ps_p[b], ("ps_p", b)

    def headnorm_T(ps, pk, dst, dkey, scale_col):
        o = (nst_n[0] % 2) * 12
        nst_n[0] += 1
        P.op("act", lambda e: e.activation(out=sq[:], in_=ps[:], func=AF.Square), reads=[pk], writes=["sq"])
        P.op("dve", lambda e: e.tensor_reduce(out=nst[:, o:o + 8], in_=sq[:].rearrange("p (h d) -> p h d", h=8), axis=AX.X, op=ALU.add),
             reads=["sq"], writes=["nst"])
        P.op("act", lambda e: e.activation(out=nst[:, o:o + 8], in_=nst[:, o:o + 8], func=AF.Sqrt, bias=eps_t[:], scale=1.0 / 64), reads=["nst", "a_eps"], writes=["nst"])
        P.op("dve", lambda e: e.reciprocal(out=nst[:, o:o + 8], in_=nst[:, o:o + 8]), reads=["nst"], writes=["nst"])
        P.op("dve", lambda e: e.tensor_tensor(out=qn[:].rearrange("p (h d) -> p h d", h=8), in0=ps[:].rearrange("p (h d) -> p h d", h=8),
                                              in1=nst[:, o:o + 8].unsqueeze(2).to_broadcast([128, 8, 64]), op=ALU.mult),
             reads=[pk, "nst"], writes=["qn"])
        for hp in range(4):
            P.op("pe", lambda e, hp=hp: e.transpose(out=norm.psT[:, hp, :], in_=qn[:, hp * 128:(hp + 1) * 128], identity=consts.ident[:]),
                 reads=["qn", "ident"], writes=[("n", "psT")])
        if scale_col is None:
            P.op("act", lambda e: e.copy(out=dst[:], in_=norm.psT[:, 0:4, :]), reads=[("n", "psT")], writes=[dkey])
        else:
            P.op("dve", lambda e: e.tensor_scalar(out=dst[:], in0=norm.psT[:, 0:4, :], scalar1=scale_col, scalar2=None, op0=ALU.mult),
                 reads=[("n", "psT"), "a_qscale"], writes=[dkey])

    nst_own = NT // 2048
    for st in range(nst_own + 1):
        halo = (st == 0)
        for s in range(16):
            P.dma("sp", x1[:], xin[st * 2048 + s * 128: st * 2048 + (s + 1) * 128, :], writes=["x1"], dkey="x1")
            hi = norm.stats_and_scale(x1[:], "x1")
            norm.transpose_to(hi, hT[:, :, s * 128:(s + 1) * 128], "hTs")
        for g, d in enumerate(DIL):
            nnb = 16 // d
            for nbl in range(nnb):
                for r in range(d):
                    if halo and nbl != nnb - 1:
                        continue
                    cols = sls(nbl * 128 * d + r, 128, d)
                    cur = free[g]
                    prv = slot[g][r]
                    kcur, vcur = ("KT", g, cur), ("VA", g, cur)
                    kprv, vprv = ("KT", g, prv), ("VA", g, prv)
                    ps, pk = proj(cols, g, 2)
                    P.op("act", lambda e, ps=ps, g=g, cur=cur: e.copy(out=VA[g][cur][:, :, 0:64], in_=ps[:].rearrange("p (h d) -> p h d", h=8)),
                         reads=[pk], writes=[vcur])
                    src1 = small["hv"] if halo else ones1
                    P.op("pool", lambda e, g=g, cur=cur, src1=src1: e.tensor_copy(out=VA[g][cur][:, :, 64:65], in_=src1[:, 0:1].unsqueeze(1).to_broadcast([128, 8, 1])),
                         reads=["a_hv", "a_ones", vcur], writes=[vcur])
                    ps, pk = proj(cols, g, 1)
                    headnorm_T(ps, pk, KT[g][cur], kcur, None)
                    if not halo:
                        ps, pk = proj(cols, g, 0)
                        headnorm_T(ps, pk, QT, "QT", qscale[:, g:g + 1])
                        for hpp in range(2):
                            sks = [("ps_s", 0), ("ps_s", 1)]
                            for hq in range(2):
                                hp = hpp * 2 + hq
                                for w, (kt, kk) in enumerate(((KT[g][prv], kprv), (KT[g][cur], kcur))):
                                    c0 = (hq * 2 + w) * 128
                                    for e_ in range(2):
                                        pr = slice(e_ * 64, (e_ + 1) * 64)
                                        P.op("pe", lambda e, kt=kt, pr=pr, hp=hp, c0=c0, e_=e_: e.matmul(out=ps_s[e_][:, c0:c0 + 128], lhsT=kt[pr, hp, :], rhs=QT[pr, hp, :], start=True, stop=True),
                                             reads=[kk, "QT"], writes=[sks[e_]])
                            for e_ in range(2):
                                pi = cnt["pt"] % 2
                                cnt["pt"] += 1
                                pk_ = ("pT", pi)
                                P.op("act", lambda e, pi=pi, e_=e_: e.activation(out=pT[pi][:], in_=ps_s[e_][:], func=AF.Exp), reads=[sks[e_]], writes=[pk_])
                                P.op("pool", lambda e, pi=pi: e.tensor_tensor(out=pT[pi][:], in0=pT[pi][:], in1=mask[:], op=ALU.mult), reads=[pk_, "a_mask"], writes=[pk_])
                                for hq in range(2):
                                    h = (hpp * 2 + hq) * 2 + e_
                                    for w, (va, vk) in enumerate(((VA[g][prv], vprv), (VA[g][cur], vcur))):
                                        c0 = (hq * 2 + w) * 128
                                        P.op("pe", lambda e, va=va, h=h, c0=c0, pi=pi, w=w: e.matmul(out=ps_o[:, h // 4, (h % 4) * 65:(h % 4) * 65 + 65], lhsT=pT[pi][:, c0:c0 + 128], rhs=va[:, h, :],
                                                                                              start=(w == 0), stop=(w == 1)),
                                             reads=[pk_, vk], writes=[("ps_o", h // 4)])
                        oi = cnt["ot"] % 2
                        cnt["ot"] += 1
                        ok_ = ("ot", oi)
                        for hb in range(2):
                            eng = ("act", "dve")[hb]
                            if eng == "act":
                                P.op("act", lambda e, hb=hb, oi=oi: e.copy(out=ot[oi][:, hb * 4:(hb + 1) * 4, :], in_=ps_o[:, hb, 0:260].rearrange("p (h d) -> p h d", h=4)),
                                     reads=[("ps_o", hb)], writes=[ok_])
                            else:
                                P.op("dve", lambda e, hb=hb, oi=oi: e.tensor_copy(out=ot[oi][:, hb * 4:(hb + 1) * 4, :], in_=ps_o[:, hb, 0:260].rearrange("p (h d) -> p h d", h=4)),
                                     reads=[("ps_o", hb)], writes=[ok_])
                        row0 = (st - 1) * 2048 + nbl * 128 * d + r
                        P.dma("sp", nd[g][sls(row0, 128, d), :], ot[oi][:].rearrange("p h d -> p (h d)"), reads=[ok_], writes=[("nd", g, st, nbl, r)], dkey=ok_)
                    slot[g][r] = cur
                    free[g] = prv
        if halo:
            continue
        for s in range(16):
            r0 = (st - 1) * 2048 + s * 128
            for g, d in enumerate(DIL):
                deps = [("nd", g, st, nbl, r) for nbl in range(16 // d) for r in range(d)]
                P.dma("sp", cmb[g][:].rearrange("p h d -> p (h d)"), nd[g][r0:r0 + 128, :], reads=deps, writes=[("cmb", g)], dkey=("cmb", g))
            P.dma("sp", xr[:], xin[HALO + r0: HALO + r0 + 128, :], writes=["xr"], dkey="xr")
            P.op("dve", lambda e: e.tensor_tensor(out=cmb[0][:], in0=cmb[0][:], in1=cmb[1][:], op=ALU.add), reads=[("cmb", 0), ("cmb", 1)], writes=[("cmb", 0)])
            P.op("dve", lambda e: e.tensor_tensor(out=cmb[0][:], in0=cmb[0][:], in1=cmb[2][:], op=ALU.add), reads=[("cmb", 0), ("cmb", 2)], writes=[("cmb", 0)])
            P.op("dve", lambda e: e.reciprocal(out=rD[:].unsqueeze(2), in_=cmb[0][:, :, 64:65]), reads=[("cmb", 0)], writes=["rD"])
            P.op("dve", lambda e: e.tensor_tensor(out=otm[:].rearrange("p (h d) -> p h d", h=8), in0=cmb[0][:, :, 0:64],
                                                  in1=rD[:].unsqueeze(2).to_broadcast([128, 8, 64]), op=ALU.mult),
                 reads=[("cmb", 0), "rD"], writes=["otm"])
            for c in range(4):
                P.op("pe", lambda e, c=c: e.transpose(out=norm.psT[:, c, :], in_=otm[:, c * 128:(c + 1) * 128], identity=consts.ident[:]),
                     reads=["otm", "ident"], writes=[("n", "psT")])
            P.op("act", lambda e: e.copy(out=oT[:], in_=norm.psT[:, 0:4, :]), reads=[("n", "psT")], writes=["oT"])
            for half in range(2):
                for c in range(4):
                    P.op("pe", lambda e, c=c, half=half: e.matmul(out=psO[:], lhsT=oT[:, c, :], rhs=wo[:, c, half * 512:(half + 1) * 512], start=(c == 0), stop=(c == 3)),
                         reads=["oT", ("wo", c, half * 512)], writes=["psO"])
                P.op("dve", lambda e, half=half: e.tensor_tensor(out=xr[:, half * 512:(half + 1) * 512], in0=psO[:], in1=xr[:, half * 512:(half + 1) * 512], op=ALU.add),
                     reads=["psO", "xr"], writes=["xr"])
            P.dma("sp", out[r0:r0 + 128, :], xr[:], reads=["xr"], writes=[("OUT", r0)], dkey="xr")
            P.outkeys.append(("OUT", r0))


def attn_mask():
    k = np.arange(128)[:, None]
    q = np.arange(128)[None, :]
    prev = (k >= q).astype(np.float32)
    cur = (k <= q).astype(np.float32)
    return np.ascontiguousarray(np.concatenate([prev, cur, prev, cur], axis=1))


def attn_host_inputs(g, w_in, q_gain, k_gain, w_out):
    c = np.ascontiguousarray
    return dict(w_in=c(w_in), w_out=c(w_out), gain=c(g.reshape(8, 128).T),
                gq=c(np.tile(q_gain.T, (2, 1))), gk=c(np.tile(k_gain.T, (2, 1))), mask=attn_mask(), ident=_ident())


def build_hgrn(NT, mode):
    assert NT % 512 == 0
    nc = bass.Bass("TRN2", target_bir_lowering=False)
    dt = lambda n, s: nc.dram_tensor(n, s, F32, kind="ExternalInput").ap()
    a = dict(xin=dt("xin", [NT, D]), w_in=dt("w_in", [D, 4096]), gain=dt("gain", [128, 8]),
             lbl=dt("lbl", [128, 32]), selw=dt("selw", [128, 4]), ident=dt("ident", [128, 128]))
    outs = {}
    if mode == "full":
        a.update(w_out=dt("w_out", [D, D]), ng=dt("ng", [128, 8]), amask=dt("amask", [64, 512]),
                 sA=dt("sA", [3, 128, 1024]), fA=dt("fA", [2, 128, 8]))
        outs["out"] = nc.dram_tensor("out", [NT, D], F32, kind="ExternalOutput").ap()
    else:
        outs["s_end"] = nc.dram_tensor("s_end", [128, 1024], F32, kind="ExternalOutput").ap()
        outs["f_tot"] = nc.dram_tensor("f_tot", [128, 8], F32, kind="ExternalOutput").ap()
    with ExitStack() as es:
        P = Prog(nc, es)
        emit_hgrn(P, NT, mode, a, outs)
        P.finish(P.outkeys)
        P.emit()
    return nc


def emit_hgrn(P, NT, mode, a, outs):
    full = (mode == "full")
    xin = a["xin"]
    consts = Consts(P, a["ident"])
    small = {}
    lst = [("gain", [128, 8]), ("lbl", [128, 32]), ("selw", [128, 4])]
    if full:
        lst += [("ng", [128, 8])]
    for nm, shp in lst:
        t = P.sb("h_" + nm, shp, F32)
        P.dma("sp", t[:], a[nm], writes=["gains" if nm in ("gain", "ng") else "h_" + nm], dkey="h_" + nm)
        small[nm] = t
    ones = P.sb("h_ones", [128, 64], F32)
    eps_t = P.sb("h_eps", [128, 1], F32)
    P.op("pool", lambda e: e.memset(ones[:], 1.0), writes=["h_ones"])
    P.op("pool", lambda e: e.memset(eps_t[:], EPS), writes=["h_eps"])
    lbw = P.sb("h_lbw", [128, 8, 4], F32)
    lbs = P.sb("h_lbs", [128, 24], F32)
    P.op("act", lambda e: e.activation(out=lbw[:].rearrange("p h l -> p (h l)"), in_=small["lbl"][:], func=AF.Exp), reads=["h_lbl"], writes=["h_lbw"])
    P.op("dve", lambda e: e.tensor_reduce(out=lbs[:, 0:8], in_=lbw[:], axis=AX.X, op=ALU.add), reads=["h_lbw"], writes=["h_lbs"])
    P.op("dve", lambda e: e.reciprocal(out=lbs[:, 0:8], in_=lbs[:, 0:8]), reads=["h_lbs"], writes=["h_lbs"])
    P.op("dve", lambda e: e.tensor_tensor(out=lbw[:], in0=lbw[:], in1=small["selw"][:].unsqueeze(1).to_broadcast([128, 8, 4]), op=ALU.mult),
         reads=["h_lbw", "h_selw"], writes=["h_lbw"])
    P.op("dve", lambda e: e.tensor_reduce(out=lbs[:, 8:16], in_=lbw[:], axis=AX.X, op=ALU.add), reads=["h_lbw"], writes=["h_lbs"])
    P.op("dve", lambda e: e.tensor_tensor(out=lbs[:, 8:16], in0=lbs[:, 8:16], in1=lbs[:, 0:8], op=ALU.mult), reads=["h_lbs"], writes=["h_lbs"])
    P.op("dve", lambda e: e.tensor_scalar(out=lbs[:, 16:24], in0=lbs[:, 8:16], scalar1=-1.0, scalar2=1.0, op0=ALU.mult, op1=ALU.add), reads=["h_lbs"], writes=["h_oml"])
    oml = lbs[:, 16:24]

    win = P.sb("win", [128, 8, 4096], BF16)
    stg = [(P.sb("wstg%d" % i, [128, 512], F32), "wstg%d" % i) for i in range(2)]
    load_cast_weight(P, "win", win, [a["w_in"][k * 128:(k + 1) * 128, :] for k in range(8)], 4096, stg,
                     gain_col=lambda r: small["gain"][:, r:r + 1], col_chunk=512)
    if full:
        wout = P.sb("wout", [128, 8, D], BF16)
        load_cast_weight(P, "wout", wout, [a["w_out"][k * 128:(k + 1) * 128, :] for k in range(8)], D, stg,
                         gain_col=lambda r: small["ng"][:, r:r + 1], col_chunk=512)
        amf = P.sb("h_amf", [64, 512], F32)
        amask = P.sb("h_amask", [64, 512], BF16)
        P.dma("sp", amf[:], a["amask"], writes=["h_amf"], dkey="h_amf")
        P.op("dve", lambda e: e.tensor_copy(out=amask[:], in_=amf[:]), reads=["h_amf"], writes=["h_amask"])

    norm = NormT(P, consts, "n", nbuf=2)
    hT = P.sb("hT", [128, 8, 512], BF16)
    x1 = [P.sb("x1_%d" % i, [128, D], F32) for i in range(2)]
    qf = P.sb("qf", [128, 512], F32)
    kk = P.sb("kk", [128, 512], F32)
    bb = P.sb("bb", [128, 512], F32)
    E = P.sb("E", [128, 512], F32)
    Ei = P.sb("Ei", [128, 512], F32)
    tmpk = P.sb("tmpk", [128, 512], F32)
    qt = P.sb("qt", [128, 8, 512], BF16)
    kt = P.sb("kt", [128, 8, 512], BF16)
    khat = P.sb("khat", [128, 8, 512], BF16)
    eLs = P.sb("eLs", [128, 8, 8], F32)
    bsum = P.sb("bsum", [128, 16], F32)
    vtm = P.sb("vtm", [64, 1024], BF16)
    khtm = P.sb("khtm", [64, 1024], BF16)
    S = P.sb("S", [128, 8, 128], F32)
    Sbf = P.sb("Sbf", [128, 8, 128], BF16)
    psQF = P.ps("psQF", [128, 2, 512])
    psS = P.ps("psS", [128, 2, 512])
    if full:
        gtm = P.sb("gtm", [64, 1024], F32)
        atm = P.sb("atm", [64, 512], BF16)
        sqj = P.sb("sqj", [64, 1024], F32)
        zf = P.sb("zf", [64, 1024], F32)
        ztm = P.sb("ztm", [64, 1024], BF16)
        zT = P.sb("zT", [128, 8, 512], BF16)
        ost = P.sb("ost", [64, 24], F32)
        xr = [P.sb("xr_%d" % i, [128, D], F32) for i in range(2)]
        tmpA = P.sb("tmpA", [128, 8, 128], F32)
        fAs = P.sb("fAs", [128, 2, 8], F32)
        psA = P.ps("psA", [64, 512])
        psO = P.ps("psO", [128, 2, 512])
    cnt = dict(x1=0, xr=0)

    if full:
        P.dma("sp", S[:].rearrange("p h v -> p (h v)"), a["sA"][0], writes=["S"], dkey="S")
        P.dma("sp", fAs[:, 0, :], a["fA"][0], writes=["fAs"], dkey="fAs0")
        P.dma("sp", fAs[:, 1, :], a["fA"][1], writes=["fAs"], dkey="fAs1")
        for i in (1, 2):
            P.dma("sp", tmpA[:].rearrange("p h v -> p (h v)"), a["sA"][i], writes=["tmpA"], dkey="tmpA")
            for h in range(8):
                P.op("dve", lambda e, h=h, i=i: e.scalar_tensor_tensor(out=S[:, h, :], in0=S[:, h, :], scalar=fAs[:, i - 1, h:h + 1], in1=tmpA[:, h, :], op0=ALU.mult, op1=ALU.add),
                     reads=["S", "fAs", "tmpA"], writes=["S"])
    else:
        P.op("pool", lambda e: e.memset(S[:], 0.0), writes=["S"])
        P.op("pool", lambda e: e.memset(bsum[:], 0.0), writes=["bsum"])
    P.op("act", lambda e: e.copy(out=Sbf[:], in_=S[:]), reads=["S"], writes=["Sbf"])

    ntiles = NT // 512
    for t in range(ntiles):
        for s in range(4):
            i = cnt["x1"] % 2
            cnt["x1"] += 1
            xk = ("x1", i)
            P.dma("sp", x1[i][:], xin[t * 512 + s * 128: t * 512 + (s + 1) * 128, :], writes=[xk], dkey=xk)
            hi = norm.stats_and_scale(x1[i][:], xk)
            norm.transpose_to(hi, hT[:, :, s * 128:(s + 1) * 128], "hT")
        for h in range(8):
            for j, cb_ in ((0, h * 128), (1, 1024 + h * 128)):
                for k in range(8):
                    P.op("pe", lambda e, k=k, j=j, cb_=cb_: e.matmul(out=psQF[:, j, :], lhsT=win[:, k, cb_:cb_ + 128], rhs=hT[:, k, :], start=(k == 0), stop=(k == 7)),
                         reads=["hT", ("win", k, (cb_ // 512) * 512)], writes=[("psQF", j)])
            P.op("act", lambda e: e.activation(out=qf[:], in_=psQF[:, 0, :], func=AF.Silu), reads=[("psQF", 0)], writes=["qf"])
            P.op("act", lambda e: e.activation(out=kk[:], in_=psQF[:, 1, :], func=AF.Sigmoid, scale=-1.0), reads=[("psQF", 1)], writes=["kk"])
            P.op("dve", lambda e, h=h: e.tensor_scalar(out=kk[:], in0=kk[:], scalar1=oml[:, h:h + 1], scalar2=None, op0=ALU.mult), reads=["kk", "h_oml"], writes=["kk"])
            P.op("act", lambda e: e.activation(out=bb[:], in_=kk[:], func=AF.Ln, bias=ones[:, 0:1], scale=-1.0), reads=["kk", "h_ones"], writes=["bb"])
            for c in range(8):
                P.op("dve", lambda e, c=c: e.tensor_tensor_scan(out=bb[:, c * 64:(c + 1) * 64], data0=ones[:, 0:64], data1=bb[:, c * 64:(c + 1) * 64], initial=0.0, op0=ALU.mult, op1=ALU.add),
                     reads=["bb", "h_ones"], writes=["bb"])
            P.op("act", lambda e: e.activation(out=E[:], in_=bb[:], func=AF.Exp), reads=["bb"], writes=["E"])
            P.op("act", lambda e: e.activation(out=Ei[:], in_=bb[:], func=AF.Exp, scale=-1.0), reads=["bb"], writes=["Ei"])
            if full:
                P.op("dve", lambda e, h=h: e.tensor_tensor(out=qt[:, h, :], in0=qf[:], in1=E[:], op=ALU.mult), reads=["qf", "E"], writes=[("qt", h)])
            P.op("dve", lambda e: e.tensor_tensor(out=tmpk[:], in0=kk[:], in1=Ei[:], op=ALU.mult), reads=["kk", "Ei"], writes=["tmpk"])
            if full:
                P.op("pool", lambda e, h=h: e.tensor_copy(out=kt[:, h, :], in_=tmpk[:]), reads=["tmpk"], writes=[("kt", h)])
            E3 = E[:].rearrange("p (c s) -> p c s", c=8)
            P.op("dve", lambda e, h=h, E3=E3: e.tensor_tensor(out=khat[:, h, :].rearrange("p (c s) -> p c s", c=8), in0=tmpk[:].rearrange("p (c s) -> p c s", c=8),
                                                       in1=E3[:, :, 63:64].to_broadcast([128, 8, 64]), op=ALU.mult),
                 reads=["tmpk", "E"], writes=[("khat", h)])
            P.op("pool", lambda e, h=h, E3=E3: e.tensor_copy(out=eLs[:, h, :].unsqueeze(2), in_=E3[:, :, 63:64]), reads=["E"], writes=[("eLs", h)])
            if not full:
                b3 = bb[:].rearrange("p (c s) -> p c s", c=8)
                P.op("dve", lambda e, h=h, b3=b3: e.tensor_reduce(out=bsum[:, 8 + h:9 + h], in_=b3[:, :, 63:64].rearrange("p c s -> p (c s)"), axis=AX.X, op=ALU.add),
                     reads=["bb"], writes=["bsum"])
                P.op("dve", lambda e, h=h: e.tensor_tensor(out=bsum[:, h:h + 1], in0=bsum[:, h:h + 1], in1=bsum[:, 8 + h:9 + h], op=ALU.add), reads=["bsum"], writes=["bsum"])
        for c in range(8):
            cs = slice(c * 64, (c + 1) * 64)
            for half in range(2):
                for k in range(8):
                    P.op("pe", lambda e, k=k, half=half, cs=cs: e.matmul(out=psQF[0:64, half, :], lhsT=hT[:, k, cs], rhs=win[:, k, 2048 + half * 512: 2048 + (half + 1) * 512], start=(k == 0), stop=(k == 7)),
                         reads=["hT", ("win", k, 2048 + half * 512)], writes=[("psQF", half)])
            P.op("act", lambda e: e.copy(out=vtm[:], in_=psQF[0:64, :, :].rearrange("p a b -> p (a b)")), reads=[("psQF", 0), ("psQF", 1)], writes=["vtm"])
            if full:
                for half in range(2):
                    for k in range(8):
                        P.op("pe", lambda e, k=k, half=half, cs=cs: e.matmul(out=psQF[0:64, half, :], lhsT=hT[:, k, cs], rhs=win[:, k, 3072 + half * 512: 3072 + (half + 1) * 512], start=(k == 0), stop=(k == 7)),
                             reads=["hT", ("win", k, 3072 + half * 512)], writes=[("psQF", half)])
                P.op("act", lambda e: e.activation(out=gtm[:], in_=psQF[0:64, :, :].rearrange("p a b -> p (a b)"), func=AF.Silu), reads=[("psQF", 0), ("psQF", 1)], writes=["gtm"])
            for h in range(8):
                P.op("pe", lambda e, h=h, cs=cs: e.transpose(out=norm.psT[0:64, h, :], in_=khat[:, h, cs], identity=consts.ident[:]),
                     reads=[("khat", h), "ident"], writes=[("n", "psT")])
            P.op("act", lambda e: e.copy(out=khtm[:], in_=norm.psT[0:64, :, :].rearrange("p a b -> p (a b)")), reads=[("n", "psT")], writes=["khtm"])
            if full:
                for h in range(8):
                    P.op("pe", lambda e, h=h, cs=cs: e.matmul(out=psA[:, h * 64:(h + 1) * 64], lhsT=kt[:, h, cs], rhs=qt[:, h, cs], start=True, stop=True),
                         reads=[("kt", h), ("qt", h)], writes=["psA"])
                P.op("dve", lambda e: e.tensor_tensor(out=atm[:], in0=psA[:], in1=amask[:], op=ALU.mult), reads=["psA", "h_amask"], writes=["atm"])
                for h in range(8):
                    P.op("pe", lambda e, h=h, cs=cs: e.matmul(out=psO[0:64, h // 4, (h % 4) * 128:(h % 4 + 1) * 128], lhsT=qt[:, h, cs], rhs=Sbf[:, h, :], start=True, stop=False),
                         reads=[("qt", h), "Sbf"], writes=[("psO", h // 4)])
                    P.op("pe", lambda e, h=h: e.matmul(out=psO[0:64, h // 4, (h % 4) * 128:(h % 4 + 1) * 128], lhsT=atm[:, h * 64:(h + 1) * 64], rhs=vtm[:, h * 128:(h + 1) * 128], start=False, stop=True),
                         reads=["atm", "vtm"], writes=[("psO", h // 4)])
            for h in range(8):
                P.op("pe", lambda e, h=h: e.matmul(out=psS[:, h // 4, (h % 4) * 128:(h % 4 + 1) * 128], lhsT=khtm[:, h * 128:(h + 1) * 128], rhs=vtm[:, h * 128:(h + 1) * 128], start=True, stop=True),
                     reads=["khtm", "vtm"], writes=[("psS", h // 4)])
            for h in range(8):
                P.op("dve", lambda e, h=h, c=c: e.scalar_tensor_tensor(out=S[:, h, :], in0=S[:, h, :], scalar=eLs[:, h, c:c + 1], in1=psS[:, h // 4, (h % 4) * 128:(h % 4 + 1) * 128], op0=ALU.mult, op1=ALU.add),
                     reads=["S", ("eLs", h), ("psS", h // 4), "Sbf"], writes=["S"])
            if full:
                P.op("pool", lambda e: e.tensor_copy(out=Sbf[:], in_=S[:]), reads=["S"], writes=["Sbf"])
                o3 = psO[0:64, :, :].rearrange("p a (h v) -> p (a h) v", h=4)
                P.op("act", lambda e: e.activation(out=sqj[:], in_=psO[0:64, :, :].rearrange("p a b -> p (a b)"), func=AF.Square), reads=[("psO", 0), ("psO", 1)], writes=["sqj"])
                P.op("dve", lambda e: e.tensor_reduce(out=ost[:, 0:8], in_=sqj[:].rearrange("p (h v) -> p h v", h=8), axis=AX.X, op=ALU.add), reads=["sqj"], writes=["ost"])
                P.op("act", lambda e: e.activation(out=ost[:, 8:16], in_=ost[:, 0:8], func=AF.Sqrt, bias=eps_t[0:64, :], scale=1.0 / 128), reads=["ost", "h_eps"], writes=["ost"])
                P.op("dve", lambda e: e.reciprocal(out=ost[:, 16:24], in_=ost[:, 8:16]), reads=["ost"], writes=["ost"])
                P.op("dve", lambda e, o3=o3: e.tensor_tensor(out=zf[:].rearrange("p (h v) -> p h v", h=8), in0=o3, in1=ost[:, 16:24].unsqueeze(2).to_broadcast([64, 8, 128]), op=ALU.mult),
                     reads=[("psO", 0), ("psO", 1), "ost"], writes=["zf"])
                P.op("pool", lambda e: e.tensor_tensor(out=ztm[:], in0=zf[:], in1=gtm[:], op=ALU.mult), reads=["zf", "gtm"], writes=["ztm"])
                for c8 in range(8):
                    P.op("pe", lambda e, c8=c8: e.transpose(out=norm.psT[:, c8, 0:64], in_=ztm[:, c8 * 128:(c8 + 1) * 128], identity=consts.ident[0:64, 0:64]),
                         reads=["ztm", "ident"], writes=[("n", "psT")])
                P.op("act", lambda e, cs=cs: e.copy(out=zT[:, :, cs], in_=norm.psT[:, :, 0:64]), reads=[("n", "psT")], writes=[("zT", c // 2)])
        if not full:
            continue
        for s in range(4):
            i = cnt["xr"] % 2
            cnt["xr"] += 1
            xk = ("xr", i)
            r0 = t * 512 + s * 128
            P.dma("sp", xr[i][:], xin[r0:r0 + 128, :], writes=[xk], dkey=xk)
            for half in range(2):
                for c8 in range(8):
                    P.op("pe", lambda e, c8=c8, s=s, half=half: e.matmul(out=psO[:, half, :], lhsT=zT[:, c8, s * 128:(s + 1) * 128], rhs=wout[:, c8, half * 512:(half + 1) * 512], start=(c8 == 0), stop=(c8 == 7)),
                         reads=[("zT", s), ("wout", c8, half * 512)], writes=[("psO", half)])
                P.op("dve", lambda e, i=i, half=half: e.tensor_tensor(out=xr[i][:, half * 512:(half + 1) * 512], in0=psO[:, half, :], in1=xr[i][:, half * 512:(half + 1) * 512], op=ALU.add),
                     reads=[("psO", half), xk], writes=[xk])
            P.dma("sp", outs["out"][r0:r0 + 128, :], xr[i][:], reads=[xk], writes=[("OUT", r0)], dkey=xk)
            P.outkeys.append(("OUT", r0))
    if not full:
        P.op("act", lambda e: e.activation(out=bsum[:, 8:16], in_=bsum[:, 0:8], func=AF.Exp), reads=["bsum"], writes=["bsum"])
        P.dma("sp", outs["f_tot"], bsum[:, 8:16], reads=["bsum"], writes=[("OUT", "f")], dkey="bsum")
        P.dma("sp", outs["s_end"], S[:].rearrange("p h v -> p (h v)"), reads=["S"], writes=[("OUT", "s")], dkey="S")
        P.outkeys += [("OUT", "f"), ("OUT", "s")]


def hgrn_amask():
    s = np.arange(64)[:, None]
    t = np.arange(64)[None, :]
    return np.ascontiguousarray(np.tile((s <= t).astype(np.float32), (1, 8)))


def hgrn_host_inputs(g, w_in, lb_logits, layer, norm_g, w_out):
    c = np.ascontiguousarray
    sel = np.zeros(4, np.float32)
    sel[1:layer + 1] = 1.0
    return dict(w_in=c(w_in), gain=c(g.reshape(8, 128).T), ng=c(norm_g.reshape(8, 128).T), w_out=c(w_out),
                lbl=c(lb_logits.reshape(4, 8, 128).transpose(2, 1, 0).reshape(128, 32)),
                selw=c(np.tile(sel[None, :], (128, 1))), amask=hgrn_amask(), ident=_ident())


NT_CORE = 4096
_PROGS = {}


def _prog(kind):
    if kind not in _PROGS:
        if kind == "ffn":
            _PROGS[kind] = build_ffn(NT_CORE)
        elif kind == "attn":
            _PROGS[kind] = build_attn(NT_CORE)
        elif kind == "conf":
            _PROGS[kind] = build_conf(NT_CORE)
        elif kind == "hgrn_state":
            _PROGS[kind] = build_hgrn(NT_CORE, "state")
        elif kind == "hgrn_full":
            _PROGS[kind] = build_hgrn(NT_CORE, "full")
    return _PROGS[kind]


def _with_halo(xf, H):
    res = []
    for c in range(NCORES):
        own = xf[c * NT_CORE:(c + 1) * NT_CORE]
        if c % 4 == 0:
            halo = np.zeros((H, D), np.float32)
        else:
            halo = xf[c * NT_CORE - H:c * NT_CORE]
        res.append(np.ascontiguousarray(np.concatenate([halo, own], axis=0)))
    return res


def _hv():
    return [np.full((128, 1), 0.0 if c % 4 == 0 else 1.0, np.float32) for c in range(NCORES)]


def _launch(kind, in_maps, name="out"):
    res = run_bass_kernel_spmd(_prog(kind), in_maps, core_ids=list(range(NCORES)))
    return res.results


def kernel(x, mixer_norm, ffn_norm, attn_w_in, attn_q_gain, attn_k_gain, attn_w_out,
           conv_w_in, conv_b_in, conv_dw_w, conv_dw_b, conv_ln_g, conv_ln_b, conv_w_out,
           conv_b_out, hgrn_w_in, hgrn_lb_logits, hgrn_norm_g, hgrn_w_out,
           ffn_w_up, ffn_conv_w, ffn_conv_b, ffn_w_down):
    f32 = lambda v: np.asarray(v, dtype=np.float32)
    xf = np.ascontiguousarray(f32(x).reshape(-1, D))
    hv = _hv()
    for layer in range(4):
        kind = layer % 3
        j = layer // 3
        g = f32(mixer_norm)[layer]
        if kind == 0:
            common = attn_host_inputs(g, f32(attn_w_in)[j], f32(attn_q_gain)[j], f32(attn_k_gain)[j], f32(attn_w_out)[j])
            xs = _with_halo(xf, HALO)
            r = _launch("attn", [dict(common, xin=xs[c], hv=hv[c]) for c in range(NCORES)])
            xf = np.concatenate([r[c]["out"] for c in range(NCORES)], axis=0)
        elif kind == 1:
            common = conf_host_inputs(g, f32(conv_w_in)[j], f32(conv_b_in)[j], f32(conv_dw_w)[j], f32(conv_dw_b)[j],
                                      f32(conv_ln_g)[j], f32(conv_ln_b)[j], f32(conv_w_out)[j], f32(conv_b_out)[j])
            xs = _with_halo(xf, 128)
            r = _launch("conf", [dict(common, xin=xs[c], hv=hv[c]) for c in range(NCORES)])
            xf = np.concatenate([r[c]["out"] for c in range(NCORES)], axis=0)
        else:
            hi = hgrn_host_inputs(g, f32(hgrn_w_in)[j], f32(hgrn_lb_logits), layer, f32(hgrn_norm_g)[j], f32(hgrn_w_out)[j])
            own = [np.ascontiguousarray(xf[c * NT_CORE:(c + 1) * NT_CORE]) for c in range(NCORES)]
            st_in = {k: v for k, v in hi.items() if k not in ("ng", "w_out", "amask")}
            r = _launch("hgrn_state", [dict(st_in, xin=own[c]) for c in range(NCORES)])
            s_end = [r[c]["s_end"] for c in range(NCORES)]
            f_tot = [r[c]["f_tot"] for c in range(NCORES)]
            maps = []
            for c in range(NCORES):
                pos = c % 4
                sA = np.zeros((3, 128, 1024), np.float32)
                fA = np.ones((2, 128, 8), np.float32)
                for i, back in enumerate((3, 2, 1)):
                    if pos >= back:
                        sA[i] = s_end[c - back]
                for i, back in enumerate((2, 1)):
                    if pos >= back:
                        fA[i] = f_tot[c - back]
                maps.append(dict(hi, xin=own[c], sA=sA, fA=fA))
            r = _launch("hgrn_full", maps)
            xf = np.concatenate([r[c]["out"] for c in range(NCORES)], axis=0)
        common = ffn_host_inputs(f32(ffn_norm)[layer], f32(ffn_w_up)[layer], f32(ffn_conv_w)[layer], f32(ffn_conv_b)[layer], f32(ffn_w_down)[layer])
        xs = _with_halo(xf, 128)
        r = _launch("ffn", [dict(common, xin=xs[c]) for c in range(NCORES)])
        xf = np.concatenate([r[c]["out"] for c in range(NCORES)], axis=0)
    return xf.reshape(2, 16384, D).astype(np.float32)
```
